# Optimizing a Trainium2 kernel written in Bass

```python
import jax, jax.numpy as jnp
from jax import lax
import numpy as np

D_MODEL = 1024
BATCH = 1
SEQ = 16384
DEPTH = 4
DEC_BATCH = 4
DEC_SEQ = 8192
PAST_LEN = 128

N_MIXERS = 2
N_CONV_LAYERS = (DEPTH + 1) // 2
N_ATTN_LAYERS = DEPTH // 2
HEAD_DIM = 128
N_HEADS = D_MODEL // HEAD_DIM
N_KV_HEADS = 2
GROUP = N_HEADS // N_KV_HEADS
ROPE_AXIS_DIM = HEAD_DIM // 2
ROPE_THETA = 10000.0
GRID_W = 64
Q_BLOCK = 128
CONV_WIDTH = 3
D_FF = 2816
LN_EPS = 1e-5
QK_EPS = 1e-6
DEEPNORM_ALPHA = (2.0 * DEPTH) ** 0.25
DEEPNORM_BETA = (8.0 * DEPTH) ** -0.25

kernel_name = "hybrid_conv_axialrope_gqa_macaron_deepnorm_encoder"


def _layer_norm(x, g, b):
    xf = x.astype(jnp.float32)
    mu = jnp.mean(xf, axis=-1, keepdims=True)
    xc = xf - mu
    var = jnp.mean(xc * xc, axis=-1, keepdims=True)
    y = xc * lax.rsqrt(var + LN_EPS) * g.astype(jnp.float32) + b.astype(jnp.float32)
    return y.astype(x.dtype)


def _swiglu(x, w_in, w_out):
    gu = x @ w_in
    g, u = jnp.split(gu, 2, axis=-1)
    return (jax.nn.silu(g) * u) @ w_out


def _short_conv_mixer(x, w_in, k, w_out):
    s = x.shape[1]
    bgate, cgate, h = jnp.split(x @ w_in, 3, axis=-1)
    v = cgate * h
    pad = CONV_WIDTH // 2
    vp = jnp.pad(v, ((0, 0), (pad, pad), (0, 0)))
    conv = k[0] * vp[:, 0:s]
    for j in range(1, CONV_WIDTH):
        conv = conv + k[j] * vp[:, j:j + s]
    return (bgate * conv) @ w_out


def _rms_head(x, g):
    xf = x.astype(jnp.float32)
    return xf * lax.rsqrt(jnp.mean(xf * xf, axis=-1, keepdims=True) + QK_EPS) * g.astype(jnp.float32)


def _rope_axis(x, pos):
    d = x.shape[-1]
    inv_freq = ROPE_THETA ** (-jnp.arange(0, d, 2, dtype=jnp.float32) / d)
    ang = pos.astype(jnp.float32)[:, None] * inv_freq[None, :]
    cos = jnp.cos(ang)[None, :, None, :]
    sin = jnp.sin(ang)[None, :, None, :]
    x1, x2 = jnp.split(x, 2, axis=-1)
    return jnp.concatenate([x1 * cos - x2 * sin, x2 * cos + x1 * sin], axis=-1)


def _axial_rope(x):
    s = x.shape[1]
    t = jnp.arange(s, dtype=jnp.int32)
    rows = t // GRID_W
    cols = t % GRID_W
    return jnp.concatenate([_rope_axis(x[..., :ROPE_AXIS_DIM], rows),
                            _rope_axis(x[..., ROPE_AXIS_DIM:], cols)], axis=-1)


def _attention_mixer(x, w_qkv, q_norm, k_norm, w_out):
    bsz, s, _ = x.shape
    qkv = x @ w_qkv
    q, k, v = jnp.split(qkv, [N_HEADS * HEAD_DIM, (N_HEADS + N_KV_HEADS) * HEAD_DIM], axis=-1)
    q = q.reshape(bsz, s, N_HEADS, HEAD_DIM)
    k = k.reshape(bsz, s, N_KV_HEADS, HEAD_DIM)
    v = v.reshape(bsz, s, N_KV_HEADS, HEAD_DIM).astype(jnp.float32)
    q = _axial_rope(_rms_head(q, q_norm)) * (HEAD_DIM ** -0.5)
    k = _axial_rope(_rms_head(k, k_norm))
    n_blk = s // Q_BLOCK
    qb = q.reshape(bsz, n_blk, Q_BLOCK, N_KV_HEADS, GROUP, HEAD_DIM).transpose(1, 0, 2, 3, 4, 5)

    def block(q_blk):
        scores = jnp.einsum('bqkgd,bskd->bkgqs', q_blk, k)
        p = jax.nn.softmax(scores, axis=-1)
        return jnp.einsum('bkgqs,bskd->bqkgd', p, v)

    o = lax.map(block, qb)
    o = o.transpose(1, 0, 2, 3, 4, 5).reshape(bsz, s, N_HEADS * HEAD_DIM).astype(x.dtype)
    return o @ w_out


def _trunk(x, ffn1_w_in, ffn1_w_out, ffn2_w_in, ffn2_w_out, ln_g, ln_b,
           conv_w_in, conv_k, conv_w_out, attn_w_qkv, attn_q_norm, attn_k_norm, attn_w_out):
    for i in range(DEPTH):
        x = _layer_norm(DEEPNORM_ALPHA * x + 0.5 * _swiglu(x, ffn1_w_in[i], ffn1_w_out[i]), ln_g[i, 0], ln_b[i, 0])
        j = i // N_MIXERS
        if i % N_MIXERS == 0:
            mix = _short_conv_mixer(x, conv_w_in[j], conv_k[j], conv_w_out[j])
        else:
            mix = _attention_mixer(x, attn_w_qkv[j], attn_q_norm[j], attn_k_norm[j], attn_w_out[j])
        x = _layer_norm(DEEPNORM_ALPHA * x + mix, ln_g[i, 1], ln_b[i, 1])
        x = _layer_norm(DEEPNORM_ALPHA * x + 0.5 * _swiglu(x, ffn2_w_in[i], ffn2_w_out[i]), ln_g[i, 2], ln_b[i, 2])
    return x


def setup_inputs(seed: int = 0) -> dict:
    key = jax.random.key(seed)
    ks = jax.random.split(key, 16)
    f32 = jnp.float32
    qkv_out = (N_HEADS + 2 * N_KV_HEADS) * HEAD_DIM

    def nrm(k, shape, scale):
        return jax.random.normal(k, shape, dtype=f32) * scale

    return {
        "x_prompt": nrm(ks[0], (BATCH, SEQ, D_MODEL), 1.0),
        "x_sample": nrm(ks[1], (DEC_BATCH, DEC_SEQ, D_MODEL), 1.0),
        "ffn1_w_in": nrm(ks[2], (DEPTH, D_MODEL, 2 * D_FF), D_MODEL ** -0.5),
        "ffn1_w_out": nrm(ks[3], (DEPTH, D_FF, D_MODEL), DEEPNORM_BETA * D_FF ** -0.5),
        "ffn2_w_in": nrm(ks[4], (DEPTH, D_MODEL, 2 * D_FF), D_MODEL ** -0.5),
        "ffn2_w_out": nrm(ks[5], (DEPTH, D_FF, D_MODEL), DEEPNORM_BETA * D_FF ** -0.5),
        "ln_g": 1.0 + nrm(ks[6], (DEPTH, 3, D_MODEL), 0.02),
        "ln_b": nrm(ks[7], (DEPTH, 3, D_MODEL), 0.02),
        "conv_w_in": nrm(ks[8], (N_CONV_LAYERS, D_MODEL, 3 * D_MODEL), D_MODEL ** -0.5),
        "conv_k": nrm(ks[9], (N_CONV_LAYERS, CONV_WIDTH, D_MODEL), CONV_WIDTH ** -0.5),
        "conv_w_out": nrm(ks[10], (N_CONV_LAYERS, D_MODEL, D_MODEL), DEEPNORM_BETA * D_MODEL ** -0.5),
        "attn_w_qkv": nrm(ks[11], (N_ATTN_LAYERS, D_MODEL, qkv_out), D_MODEL ** -0.5),
        "attn_q_norm": 1.0 + nrm(ks[12], (N_ATTN_LAYERS, HEAD_DIM), 0.02),
        "attn_k_norm": 1.0 + nrm(ks[13], (N_ATTN_LAYERS, HEAD_DIM), 0.02),
        "attn_w_out": nrm(ks[14], (N_ATTN_LAYERS, N_HEADS * HEAD_DIM, D_MODEL), DEEPNORM_BETA * (N_HEADS * HEAD_DIM) ** -0.5),
    }


def reference(x_prompt, x_sample, ffn1_w_in, ffn1_w_out, ffn2_w_in, ffn2_w_out, ln_g, ln_b,
              conv_w_in, conv_k, conv_w_out, attn_w_qkv, attn_q_norm, attn_k_norm, attn_w_out):
    y_prompt = _trunk(x_prompt, ffn1_w_in, ffn1_w_out, ffn2_w_in, ffn2_w_out, ln_g, ln_b,
                      conv_w_in, conv_k, conv_w_out, attn_w_qkv, attn_q_norm, attn_k_norm, attn_w_out)
    y_sample = _trunk(x_sample, ffn1_w_in, ffn1_w_out, ffn2_w_in, ffn2_w_out, ln_g, ln_b,
                      conv_w_in, conv_k, conv_w_out, attn_w_qkv, attn_q_norm, attn_k_norm, attn_w_out)
    return (y_prompt, y_sample)
```

```python
import numpy as np
import concourse.bass as bass
import concourse.mybir as mybir
from concourse.bass_utils import run_bass_kernel_spmd

F32 = mybir.dt.float32
BF16 = mybir.dt.bfloat16
AF = mybir.ActivationFunctionType
ALU = mybir.AluOpType
AX = mybir.AxisListType

D = 1024
DFF = 2816
KC = D // 128
FC = DFF // 128
NH = 8
NKV = 2
GRP = NH // NKV
HD = 128
NCORES = 8
LN_EPS = 1e-5
QK_EPS = 1e-6
GRID_W = 64
ROPE_THETA = 10000.0


class Cfg:
    def __init__(self, T=1024, NPT=2, NSB=4, DEPTH=4):
        self.T = T
        self.NPT = NPT
        self.NSB = NSB
        self.DEPTH = DEPTH
        self.NT = NPT + NSB
        self.NB = T // 512
        self.S_PROMPT = NCORES * NPT * T
        self.S_SAMPLE = NCORES * T
        self.ALPHA = (2.0 * DEPTH) ** 0.25
        self.NCONV = (DEPTH + 1) // 2
        self.NATT = DEPTH // 2


class Sem:
    registry = []

    def __init__(self, nc, name):
        self.h = nc.alloc_semaphore(name)
        self.count = 0
        self.name = name
        Sem.registry.append(self)


class Tok:
    __slots__ = ("sem", "val")

    def __init__(self, sem, val):
        self.sem = sem
        self.val = val


class Buf:
    __slots__ = ("w", "r", "name", "excl")

    def __init__(self, name="", excl=False):
        self.w = None
        self.r = {}
        self.name = name
        self.excl = excl


class Eng:
    def __init__(self, nc, name, e, safe=False):
        self.e = e
        self.name = name
        self.sem = Sem(nc, "prog_" + name)
        self.waited = {}
        self.safe = safe

    def wait(self, tok):
        if tok is None:
            return
        if tok.sem is self.sem and self.safe:
            return
        if self.waited.get(tok.sem, 0) >= tok.val:
            return
        self.e.wait_ge(tok.sem.h, tok.val)
        self.waited[tok.sem] = tok.val


def _waits(E, reads, writes):
    for b in reads:
        E.wait(b.w)
    for b in writes:
        E.wait(b.w)
        for sem, val in list(b.r.items()):
            E.wait(Tok(sem, val))


def _record(tok, reads, writes):
    for b in reads:
        if b.r.get(tok.sem, 0) < tok.val:
            b.r[tok.sem] = tok.val
    for b in writes:
        b.w = tok
        b.r = {}


def op(E, fn, reads=(), writes=(), signal=True):
    ex = [b for b in reads if b.excl]
    if ex:
        reads = [b for b in reads if not b.excl]
        writes = list(writes) + ex
    _waits(E, reads, writes)
    ins = fn(E.e)
    if signal:
        E.sem.count += 1
        ins.then_inc(E.sem.h, 1)
        tok = Tok(E.sem, E.sem.count)
    else:
        tok = Tok(E.sem, E.sem.count + 1)
    _record(tok, reads, writes)
    return tok


def dma(Q, out, in_, sem, reads=(), writes=(), batch=False, **kw):
    _waits(Q, reads, writes)
    if not batch and sem.count > 0:
        Q.wait(Tok(sem, sem.count))
    ins = Q.e.dma_start(out=out, in_=in_, **kw)
    sem.count += 16
    ins.then_inc(sem.h, 16)
    tok = Tok(sem, sem.count)
    _record(tok, reads, writes)
    return tok


def build(cfg):
    Sem.registry = []
    T, NB, NT, DEPTH = cfg.T, cfg.NB, cfg.NT, cfg.DEPTH
    NPT, NSB = cfg.NPT, cfg.NSB
    NTOK = NT * T
    ALPHA = cfg.ALPHA
    nc = bass.Bass("TRN2", target_bir_lowering=False)
    sem_snap = nc.snapshot_sems()

    in_names = []
    _stage = getattr(cfg, "stage", 0)

    def din(name, shape, dt=F32):
        if _stage in (-1, 1) and ("w_" in name):
            return nc.dram_tensor(name, list(shape), dt).ap()
        in_names.append(name)
        return nc.dram_tensor(name, list(shape), dt, kind="ExternalInput").ap()

    def dscr(name, shape, dt):
        return nc.dram_tensor(name, list(shape), dt).ap()

    xin = din("xin", [NTOK, D])
    yout = nc.dram_tensor("yout", [NTOK, D], F32, kind="ExternalOutput").ap()
    w_f1i = din("ffn1_w_in", [DEPTH, D, 2 * DFF])
    w_f1o = din("ffn1_w_out", [DEPTH, DFF, D])
    w_f2i = din("ffn2_w_in", [DEPTH, D, 2 * DFF])
    w_f2o = din("ffn2_w_out", [DEPTH, DFF, D])
    w_ci = din("conv_w_in", [cfg.NCONV, D, 3 * D])
    w_co = din("conv_w_out", [cfg.NCONV, D, D])
    w_qkv = din("attn_w_qkv", [cfg.NATT, D, 1536])
    w_qkp = din("attn_w_qkp", [cfg.NATT, D, 1280])
    w_ao = din("attn_w_out", [cfg.NATT, D, D])
    lng_in = din("lng", [128, DEPTH * 3 * KC])
    lnb_in = din("lnb", [128, DEPTH * 3 * KC])
    convk_in = din("convk", [128, cfg.NCONV * 3 * KC])
    qkn_in = din("qkn", [128, cfg.NATT * 4])
    qkrow_in = din("qkrow", [128, cfg.NATT * 2 * 128])
    cos_in = din("ropecos", [128, NTOK])
    sin_in = din("ropesin", [128, NTOK])
    hmask_in = din("hmask", [NCORES * 2 * NT, 2 * NT])
    ident_in = din("ident", [128, 128])

    wb_f1i = dscr("wb_f1i", [DEPTH, D, 2 * DFF], BF16)
    wb_f1o = dscr("wb_f1o", [DEPTH, DFF, D], BF16)
    wb_f2i = dscr("wb_f2i", [DEPTH, D, 2 * DFF], BF16)
    wb_f2o = dscr("wb_f2o", [DEPTH, DFF, D], BF16)
    wb_ci = dscr("wb_ci", [cfg.NCONV, D, 3 * D], BF16)
    wb_co = dscr("wb_co", [cfg.NCONV, D, D], BF16)
    wb_qkv = dscr("wb_qkv", [cfg.NATT, D, 1536], BF16)
    wb_qkp = dscr("wb_qkp", [cfg.NATT, D, 1280], BF16)
    wb_ao = dscr("wb_ao", [cfg.NATT, D, D], BF16)
    xsp = dscr("xsp", [NT, 128, KC * T], F32)
    vsp = dscr("vsp", [NT, 128, KC * T], F32)
    vb_loc = dscr("vb_loc", [2 * NT, D], F32)
    vb_gat = dscr("vb_gat", [NCORES * 2 * NT, D], F32)
    qsp = dscr("qsp", [NT, 128, NH * T], BF16)
    osp = dscr("osp", [NT, 128, NH * T], BF16)
    kt_loc = dscr("kt_loc", [128, NT * NKV * T], BF16)
    kt_gat = dscr("kt_gat", [NCORES * 128, NT * NKV * T], BF16)
    v_loc = dscr("v_loc", [128, NT * T * 2], BF16)
    v_gat = dscr("v_gat", [NCORES * 128, NT * T * 2], BF16)
    kt_loc_v = kt_loc.rearrange("i c -> (i c)").rearrange("(q p n) -> q p n", p=128, n=T)
    v_loc_v = v_loc.rearrange("i c -> (i c)").rearrange("(t g p c) -> t p g c", t=NT, g=T // 128, p=128)

    PE = Eng(nc, "pe", nc.tensor, safe=True)
    ACT = Eng(nc, "act", nc.scalar)
    DVE = Eng(nc, "dve", nc.vector)
    POOL = Eng(nc, "pool", nc.gpsimd)
    SP = Eng(nc, "sp", nc.sync)
    ENGS = [PE, ACT, DVE, POOL, SP]

    def barrier():
        toks = [Tok(E.sem, E.sem.count) for E in (PE, ACT, DVE) if E.sem.count > 0]
        for sm in Sem.registry:
            if sm.name.startswith("s_") and not sm.name.startswith("s_c_") and sm.count > 0 and sm.name != "s_cc":
                toks.append(Tok(sm, sm.count))
        for E in (PE, ACT, DVE, SP):
            for t in toks:
                if t.sem is not E.sem:
                    E.wait(t)

    off = [0]

    def sb(name, shape, dt, at=None):
        nbytes = int(np.prod(shape[1:])) * (4 if dt == F32 else 2)
        if at is None:
            o = off[0]
            off[0] += (nbytes + 63) // 64 * 64
        else:
            o = at
        return nc.alloc_sbuf_tensor_at(name, list(shape), dt, offset=o), o, nbytes

    base0 = nc.SBUF_PARTITION_SIZE_BYTES - nc.sbuf_bytes_remaining
    off[0] = (base0 + 63) // 64 * 64
    ident, _, _ = sb("ident", [128, 128], F32)
    ones_bf, _, _ = sb("ones_bf", [128, 128], BF16)
    lng, _, _ = sb("lng", [128, DEPTH * 3 * KC], F32)
    lnb, _, _ = sb("lnb", [128, DEPTH * 3 * KC], F32)
    convk, _, _ = sb("convk", [128, cfg.NCONV * 3 * KC], F32)
    qkn, _, _ = sb("qkn", [128, cfg.NATT * 4], F32)
    negc, _, _ = sb("negc", [128, cfg.NATT], F32)
    epsc, _, _ = sb("epsc", [128, 4], F32)
    halo, _, _ = sb("halo", [128, KC * 2 * NT], F32)
    sqb = [sb("sqb%d" % i, [128, 512], BF16)[0] for i in range(2)]
    ybb = [sb("ybb%d" % i, [128, 512], BF16)[0] for i in range(2)]
    mean_t = [sb("mean%d" % i, [128, 512], F32)[0] for i in range(NB)]
    rstd_t = [sb("rstd%d" % i, [128, 512], F32)[0] for i in range(NB)]
    tmp_t = [sb("tmp%d" % i, [128, 512], F32)[0] for i in range(2)]
    phase_base = off[0]
    xf, _, _ = sb("xf", [128, KC, T], F32)
    xb, _, _ = sb("xb", [128, KC, T], BF16)
    hb, hb_off, hb_bytes = sb("hb", [128, FC, T], BF16)
    sg = [sb("sg%d" % i, [128, 512], F32)[0] for i in range(2)]
    WA_N = 4
    wa = [sb("wa%d" % i, [128, KC, 512], BF16)[0] for i in range(WA_N)]
    WB_N = 2
    _wbr = [sb("wbr%d" % i, [128, FC, 256], BF16) for i in range(WB_N)]
    wbr = [w[0] for w in _wbr]
    wbr_off = _wbr[0][1]
    ffn_end = off[0]
    tmaj = nc.alloc_sbuf_tensor_at("tmaj", [128, T // 128, D], F32, offset=hb_off)
    assert (T // 128) * D * 4 <= hb_bytes
    vh = nc.alloc_sbuf_tensor_at("vh", [128, KC, T + 2], F32, offset=hb_off)
    assert KC * (T + 2) * 4 <= hb_bytes
    cos_t = nc.alloc_sbuf_tensor_at("cos_t", [128, T], F32, offset=hb_off)
    sin_t = nc.alloc_sbuf_tensor_at("sin_t", [128, T], F32, offset=hb_off + T * 4)
    qstage = nc.alloc_sbuf_tensor_at("qstage", [128, NH, T], BF16, offset=hb_off + 2 * T * 4)
    kstage = nc.alloc_sbuf_tensor_at("kstage", [128, NKV, T], BF16, offset=hb_off + 2 * T * 4 + NH * T * 2)
    vstage = nc.alloc_sbuf_tensor_at("vstage", [128, T // 128, NKV * HD], BF16,
                                     offset=hb_off + 2 * T * 4 + (NH + NKV) * T * 2)
    ropA = nc.alloc_sbuf_tensor_at("ropA", [128, 512], F32, offset=hb_off + 2 * T * 4 + (NH + NKV) * T * 2 + (T // 128) * 512)
    ropB = nc.alloc_sbuf_tensor_at("ropB", [128, 512], F32, offset=hb_off + 2 * T * 4 + (NH + NKV) * T * 2 + (T // 128) * 512 + 2048)
    assert 2 * T * 4 + (NH + NKV) * T * 2 + (T // 128) * 512 + 4096 <= hb_bytes
    oin = nc.alloc_sbuf_tensor_at("oin", [128, NH, T], BF16, offset=hb_off)
    off[0] = phase_base
    SMAX = cfg.S_PROMPT
    NQMAX = NPT * T
    KT = [sb("KT%d" % i, [128, SMAX], BF16)[0] for i in range(2)]
    VS = [sb("VS%d" % i, [128, SMAX // 128, HD], BF16)[0] for i in range(2)]
    qT, _, _ = sb("qT", [128, GRP, NQMAX], BF16)
    oT, _, _ = sb("oT", [128, GRP, NQMAX], BF16)
    PT_N = 3
    pT = [sb("pT%d" % i, [128, 2, 512], BF16)[0] for i in range(PT_N)]
    rec = [sb("rec%d" % i, [128, 512], F32)[0] for i in range(2)]
    att_end = off[0]
    gst, _, _ = sb("gst", [NCORES * 2 * NT, D], F32, at=max(ffn_end, att_end))
    hm, _, _ = sb("hm", [NCORES * 2 * NT, 2 * NT], F32, at=max(ffn_end, att_end) + D * 4)
    total_end = max(ffn_end, att_end) + D * 4 + 2 * NT * 4
    assert total_end <= nc.SBUF_PARTITION_SIZE_BYTES, (total_end, nc.SBUF_PARTITION_SIZE_BYTES)

    PS = [nc.alloc_psum_tensor("ps%d" % i, [128, 1024], F32) for i in range(4)]
    bank_ap = [PS[i // 2][:, (i % 2) * 512:(i % 2 + 1) * 512] for i in range(8)]
    bankB = [Buf("bank%d" % i, excl=True) for i in range(8)]

    B_xf = [[Buf() for _ in range(NB)] for _ in range(KC)]
    B_xb = [[Buf() for _ in range(NB)] for _ in range(KC)]
    B_hb = [[Buf() for _ in range(NB)] for _ in range(FC)]
    B_hball = Buf("hball")
    B_sg = [Buf() for _ in range(2)]
    B_wa = [Buf() for _ in range(WA_N)]
    B_wb = [Buf() for _ in range(WB_N)]
    S_wa = [Sem(nc, "s_wa%d" % i) for i in range(WA_N)]
    S_wb = [Sem(nc, "s_wb%d" % i) for i in range(WB_N)]
    B_sq = [Buf() for _ in range(2)]
    B_yb = [Buf() for _ in range(2)]
    B_mean = [Buf() for _ in range(NB)]
    B_rstd = [Buf() for _ in range(NB)]
    B_tmp = [Buf() for _ in range(2)]
    B_const = Buf("const")
    S_const = Sem(nc, "s_const")
    S_x = Sem(nc, "s_x")
    S_st = Sem(nc, "s_st")
    S_st2 = Sem(nc, "s_st2")
    S_st3 = Sem(nc, "s_st3")
    S_ld2 = Sem(nc, "s_ld2")
    S_ld3 = Sem(nc, "s_ld3")
    S_cc = Sem(nc, "s_cc")
    B_xsp = [Buf() for _ in range(NT)]
    B_vsp = [Buf() for _ in range(NT)]
    B_qsp = [Buf() for _ in range(NT)]
    B_osp = [Buf() for _ in range(NT)]
    B_vbloc = Buf()
    B_vbgat = Buf()
    B_ktloc = Buf()
    B_ktgat = Buf()
    B_vloc = Buf()
    B_vgat = Buf()
    B_halo = Buf()
    B_gst = Buf()
    B_KT = [Buf() for _ in range(2)]
    B_VS = [Buf() for _ in range(2)]
    S_KT = [Sem(nc, "s_kt%d" % i) for i in range(2)]
    S_VS = [Sem(nc, "s_vs%d" % i) for i in range(2)]
    B_qT = Buf()
    S_qT = Sem(nc, "s_qT")
    B_oT = Buf()
    B_pT = [Buf() for _ in range(PT_N)]
    B_rec = [Buf() for _ in range(2)]
    B_negc = Buf()

    wcast = {}

    def cast_weight(key, src2d, dst2d, nrows):
        if key in wcast:
            return
        b = Buf(key)
        s = Sem(nc, "s_c_" + key)
        r = 0
        k = 0
        while r < nrows:
            n = min(256, nrows - r)
            if k >= 2:
                POOL.e.wait_ge(s.h, 16 * (k - 1))
            dma(POOL, dst2d[r:r + n, :], src2d[r:r + n, :], s, writes=[b], batch=True)
            r += n
            k += 1
        wcast[key] = b

    def cast_layer_weights(l):
        cast_weight("f1i%d" % l, w_f1i[l], wb_f1i[l], D)
        cast_weight("f1o%d" % l, w_f1o[l], wb_f1o[l], DFF)
        j = l // 2
        if l % 2 == 0:
            cast_weight("ci%d" % j, w_ci[j], wb_ci[j], D)
            cast_weight("co%d" % j, w_co[j], wb_co[j], D)
        else:
            cast_weight("qkv%d" % j, w_qkv[j], wb_qkv[j], D)
            cast_weight("qkp%d" % j, w_qkp[j], wb_qkp[j], D)
            cast_weight("ao%d" % j, w_ao[j], wb_ao[j], D)
        cast_weight("f2i%d" % l, w_f2i[l], wb_f2i[l], D)
        cast_weight("f2o%d" % l, w_f2o[l], wb_f2o[l], DFF)

    for (dst, src) in [(ident, ident_in), (lng, lng_in), (lnb, lnb_in), (convk, convk_in), (qkn, qkn_in)]:
        dma(SP, dst[:], src[:, :], S_const, writes=[B_const], batch=True)
    dma(SP, hm[:], hmask_in[:, :], S_const, writes=[B_const], batch=True)
    op(DVE, lambda e: e.memset(ones_bf[:], 1.0), writes=[B_const])
    op(DVE, lambda e: e.memset(epsc[:, 0:1], LN_EPS), writes=[B_const])
    op(DVE, lambda e: e.memset(epsc[:, 1:2], float(HD * QK_EPS)), writes=[B_const])
    op(DVE, lambda e: e.memset(epsc[:, 2:3], float(QK_EPS)), writes=[B_const])
    if not getattr(cfg, 'nocast', False):
        cast_layer_weights(0)
    for a in range(0 if getattr(cfg, 'nonegc', False) else cfg.NATT):
        r0 = tmp_t[0]
        dma(SP, r0[:, 0:256], qkrow_in[:, a * 256:(a + 1) * 256], S_ld3, writes=[B_tmp[0]])
        op(DVE, lambda e: e.tensor_reduce(out=r0[:, 256:257], in_=r0[:, 0:128], axis=AX.X, op=ALU.max,
                                          apply_absolute_value=True), reads=[B_tmp[0]], writes=[B_tmp[1]])
        op(DVE, lambda e: e.tensor_reduce(out=r0[:, 257:258], in_=r0[:, 128:256], axis=AX.X, op=ALU.max,
                                          apply_absolute_value=True), reads=[B_tmp[0]], writes=[B_tmp[1]])
        op(DVE, lambda e: e.tensor_tensor(out=r0[:, 258:259], in0=r0[:, 256:257], in1=r0[:, 257:258], op=ALU.mult),
           reads=[B_tmp[1]], writes=[B_tmp[1]])
        op(DVE, lambda e: e.tensor_scalar(out=negc[:, a:a + 1], in0=r0[:, 258:259], scalar1=-float(np.sqrt(128.0)),
                                          scalar2=None, op0=ALU.mult), reads=[B_tmp[1]], writes=[B_negc, B_tmp[0]])

    bank_rr = [0]
    pair_rr = [0]

    def next_bank(n=1):
        if n == 2:
            p = pair_rr[0] % 2
            pair_rr[0] += 1
            return 2 * p, 2 * p + 1
        b = bank_rr[0] % 4
        bank_rr[0] += 1
        return b

    wa_rr = [0]
    wb_rr = [0]

    def load_wa(wd2d, wbuf, col0, ncols):
        i = wa_rr[0] % WA_N
        wa_rr[0] += 1
        dma(SP, wa[i][:, :, 0:ncols], wd2d[:, col0:col0 + ncols].rearrange("(j p) c -> p j c", p=128), S_wa[i],
            reads=[wbuf], writes=[B_wa[i]])
        return wa[i], B_wa[i]

    def load_wb(wd2d, wbuf, col0, ncols):
        i = wb_rr[0] % WB_N
        wb_rr[0] += 1
        dma(SP, wbr[i][:, :, 0:ncols], wd2d[:, col0:col0 + ncols].rearrange("(f p) c -> p f c", p=128), S_wb[i],
            reads=[wbuf], writes=[B_wb[i]])
        return wbr[i], B_wb[i]

    def mm_group(bank, lhs_list, rhs_list, reads):
        n = len(lhs_list)
        _waits(PE, reads, [bankB[bank]])
        for k in range(n):
            ins = PE.e.matmul(bank_ap[bank], lhsT=lhs_list[k], rhs=rhs_list[k], start=(k == 0), stop=(k == n - 1))
        PE.sem.count += 1
        ins.then_inc(PE.sem.h, 1)
        _record(Tok(PE.sem, PE.sem.count), reads, [bankB[bank]])

    class LNState:
        pass

    def ln_accumulate(st, m, nb, scratch_i):
        sl = slice(nb * 512, (nb + 1) * 512)
        i = scratch_i % 2
        op(ACT, lambda e: e.activation(out=sqb[i][:], in_=xf[:, m, sl], func=AF.Square), reads=[B_xf[m][nb]],
           writes=[B_sq[i]])
        op(ACT, lambda e: e.activation(out=ybb[i][:], in_=xf[:, m, sl], func=AF.Copy), reads=[B_xf[m][nb]],
           writes=[B_yb[i]])

        def stats():
            op(PE, lambda e: e.matmul(bank_ap[4 + nb], lhsT=ones_bf[:], rhs=ybb[i][:], start=(m == 0), stop=(m == KC - 1)),
               reads=[B_yb[i], B_const], writes=[bankB[4 + nb]], signal=True)
            op(PE, lambda e: e.matmul(bank_ap[6 + nb], lhsT=ones_bf[:], rhs=sqb[i][:], start=(m == 0), stop=(m == KC - 1)),
               reads=[B_sq[i], B_const], writes=[bankB[6 + nb]], signal=True)
        st.deferred.append(stats)

    def ln_finish(st, l, k, final=False):
        for f in st.deferred:
            f()
        st.deferred = []
        gi = (l * 3 + k) * KC
        for nb in range(NB):
            op(DVE, lambda e: e.tensor_scalar(out=mean_t[nb][:], in0=bank_ap[4 + nb], scalar1=1.0 / D, scalar2=None,
                                              op0=ALU.mult), reads=[bankB[4 + nb]], writes=[B_mean[nb]])
            op(DVE, lambda e: e.tensor_tensor(out=tmp_t[0][:], in0=mean_t[nb][:], in1=mean_t[nb][:], op=ALU.mult),
               reads=[B_mean[nb]], writes=[B_tmp[0]])
            op(DVE, lambda e: e.scalar_tensor_tensor(out=tmp_t[1][:], in0=bank_ap[6 + nb], scalar=1.0 / D, in1=tmp_t[0][:],
                                                     op0=ALU.mult, op1=ALU.subtract),
               reads=[bankB[6 + nb], B_tmp[0]], writes=[B_tmp[1]])
            op(ACT, lambda e: e.activation(out=tmp_t[1][:], in_=tmp_t[1][:], func=AF.Sqrt, bias=epsc[:, 0:1], scale=1.0),
               reads=[B_tmp[1], B_const], writes=[B_tmp[1]])
            op(DVE, lambda e: e.reciprocal(out=rstd_t[nb][:], in_=tmp_t[1][:]), reads=[B_tmp[1]], writes=[B_rstd[nb]])
        if _stage == 2 and getattr(cfg, "sub", 0) == 3:
            return
        for nb in range(NB):
            sl = slice(nb * 512, (nb + 1) * 512)
            for m in range(KC):
                op(DVE, lambda e: e.tensor_tensor(out=xf[:, m, sl], in0=xf[:, m, sl], in1=mean_t[nb][:], op=ALU.subtract),
                   reads=[B_mean[nb]], writes=[B_xf[m][nb]])
                op(DVE, lambda e: e.tensor_tensor(out=xf[:, m, sl], in0=xf[:, m, sl], in1=rstd_t[nb][:], op=ALU.mult),
                   reads=[B_rstd[nb]], writes=[B_xf[m][nb]])
                op(DVE, lambda e: e.tensor_scalar(out=xf[:, m, sl], in0=xf[:, m, sl], scalar1=lng[:, gi + m:gi + m + 1],
                                                  scalar2=lnb[:, gi + m:gi + m + 1], op0=ALU.mult, op1=ALU.add),
                   reads=[B_const], writes=[B_xf[m][nb]])
                op(ACT, lambda e: e.activation(out=xb[:, m, sl], in_=xf[:, m, sl], func=AF.Copy),
                   reads=[B_xf[m][nb]], writes=[B_xb[m][nb]])

    def residual_from_bank(st, bank, m, nb, cnt):
        sl = slice(nb * 512, (nb + 1) * 512)
        op(DVE, lambda e: e.scalar_tensor_tensor(out=xf[:, m, sl], in0=xf[:, m, sl], scalar=float(ALPHA), in1=bank_ap[bank],
                                                 op0=ALU.mult, op1=ALU.add),
           reads=[bankB[bank]], writes=[B_xf[m][nb]])
        ln_accumulate(st, m, nb, cnt)

    def new_ln():
        st = LNState()
        st.deferred = []
        return st

    def run_deferred(st, keep=1):
        while len(st.deferred) > keep:
            st.deferred.pop(0)()

    def ffn(l, which):
        wi = (wb_f1i if which == 1 else wb_f2i)[l]
        wo = (wb_f1o if which == 1 else wb_f2o)[l]
        bi = wcast[("f1i%d" if which == 1 else "f2i%d") % l]
        bo = wcast[("f1o%d" if which == 1 else "f2o%d") % l]
        k_ln = 0 if which == 1 else 2
        c = 0
        sgi = 0
        while c < FC:
            n = min(4, FC - c)
            gs, gB = load_wa(wi, bi, c * 128, n * 128)
            us, uB = load_wa(wi, bi, DFF + c * 128, n * 128)
            for cc in range(n):
                for nb in range(NB):
                    sl = slice(nb * 512, (nb + 1) * 512)
                    bg, bu = next_bank(2)
                    mm_group(bg, [gs[:, j, cc * 128:(cc + 1) * 128] for j in range(KC)], [xb[:, j, sl] for j in range(KC)],
                             reads=[gB] + [B_xb[j][nb] for j in range(KC)])
                    mm_group(bu, [us[:, j, cc * 128:(cc + 1) * 128] for j in range(KC)], [xb[:, j, sl] for j in range(KC)],
                             reads=[uB] + [B_xb[j][nb] for j in range(KC)])
                    s = sgi % 2
                    sgi += 1
                    op(ACT, lambda e: e.activation(out=sg[s][:], in_=bank_ap[bg], func=AF.Silu), reads=[bankB[bg]],
                       writes=[B_sg[s]])
                    fch = c + cc
                    op(DVE, lambda e: e.scalar_tensor_tensor(out=hb[:, fch, sl], in0=sg[s][:], scalar=0.5, in1=bank_ap[bu],
                                                             op0=ALU.mult, op1=ALU.mult),
                       reads=[B_sg[s], bankB[bu]], writes=[B_hb[fch][nb], B_hball])
            c += n
        if _stage == 2 and getattr(cfg, "sub", 0) == 1:
            return
        st = new_ln()
        cnt = 0
        for ms in range(0, KC, 2):
            ws, wB = load_wb(wo, bo, ms * 128, 256)
            for mm in range(2):
                m = ms + mm
                for nb in range(NB):
                    sl = slice(nb * 512, (nb + 1) * 512)
                    b = next_bank()
                    mm_group(b, [ws[:, f, mm * 128:(mm + 1) * 128] for f in range(FC)], [hb[:, f, sl] for f in range(FC)],
                             reads=[wB] + [B_hb[f][nb] for f in range(FC)])
                    run_deferred(st, keep=1)
                    residual_from_bank(st, b, m, nb, cnt)
                    cnt += 1
        if _stage == 2 and getattr(cfg, "sub", 0) == 2:
            for f in st.deferred:
                f()
            return
        ln_finish(st, l, k_ln)

    def load_tile_input(t):
        dma(SP, tmaj[:], xin[t * T:(t + 1) * T, :].rearrange("(g p) d -> p g d", p=128), S_x,
            writes=[B_hball] + [B_hb[f][nb] for f in range(FC) for nb in range(NB)])
        for j in range(KC):
            for nb in range(NB):
                b = next_bank()
                for g in range(4):
                    grp = nb * 4 + g
                    op(PE, lambda e, g=g, grp=grp: e.transpose(out=bank_ap[b][:, g * 128:(g + 1) * 128],
                                                              in_=tmaj[:, grp, j * 128:(j + 1) * 128], identity=ident[:]),
                       reads=[B_hball, B_const], writes=[bankB[b]], signal=(g == 3))
                sl = slice(nb * 512, (nb + 1) * 512)
                op(ACT, lambda e: e.activation(out=xf[:, j, sl], in_=bank_ap[b], func=AF.Copy), reads=[bankB[b]],
                   writes=[B_xf[j][nb]])
                op(DVE, lambda e: e.tensor_copy(out=xb[:, j, sl], in_=xf[:, j, sl]), reads=[B_xf[j][nb]], writes=[B_xb[j][nb]])

    def store_tile_output(t):
        allhb = [B_hball] + [B_hb[f][nb] for f in range(FC) for nb in range(NB)]
        first = True
        for grp in range(T // 128):
            nb = grp // 4
            for jh in range(2):
                b = next_bank()
                for jj in range(4):
                    j = jh * 4 + jj
                    op(PE, lambda e, jj=jj, j=j: e.transpose(out=bank_ap[b][:, jj * 128:(jj + 1) * 128],
                                                            in_=xf[:, j, grp * 128:(grp + 1) * 128], identity=ident[:]),
                       reads=[B_xf[j][nb], B_const], writes=[bankB[b]], signal=(jj == 3))
                if jh == 0:
                    op(ACT, lambda e: e.activation(out=tmaj[:, grp, 0:512], in_=bank_ap[b], func=AF.Copy),
                       reads=[bankB[b]], writes=(allhb if first else [B_hball]))
                else:
                    op(DVE, lambda e: e.tensor_copy(out=tmaj[:, grp, 512:1024], in_=bank_ap[b]), reads=[bankB[b]],
                       writes=[B_hball])
                first = False
        dma(SP, yout[t * T:(t + 1) * T, :].rearrange("(g p) d -> p g d", p=128), tmaj[:], S_st, reads=[B_hball])

    def all_xf():
        return [B_xf[j][nb] for j in range(KC) for nb in range(NB)]

    def all_xb():
        return [B_xb[j][nb] for j in range(KC) for nb in range(NB)]

    def spill_x(t):
        dma(SP, xsp[t].rearrange("p (j n) -> p j n", j=KC), xf[:], S_st2, reads=all_xf(), writes=[B_xsp[t]])

    def reload_x(t):
        dma(SP, xf[:], xsp[t].rearrange("p (j n) -> p j n", j=KC), S_x, reads=[B_xsp[t]], writes=all_xf())
        for j in range(KC):
            for nb in range(NB):
                sl = slice(nb * 512, (nb + 1) * 512)
                op(ACT, lambda e: e.activation(out=xb[:, j, sl], in_=xf[:, j, sl], func=AF.Copy), reads=[B_xf[j][nb]],
                   writes=[B_xb[j][nb]])

    def conv_in(t, lc):
        wi = wb_ci[lc]
        bi = wcast["ci%d" % lc]
        allhb = [B_hball] + [B_hb[f][nb] for f in range(FC) for nb in range(NB)]
        first = True
        for js in range(0, KC, 4):
            cs, cB = load_wa(wi, bi, D + js * 128, 512)
            hs, hB = load_wa(wi, bi, 2 * D + js * 128, 512)
            for jj in range(4):
                j = js + jj
                for nb in range(NB):
                    sl = slice(nb * 512, (nb + 1) * 512)
                    bc, bh = next_bank(2)
                    mm_group(bc, [cs[:, k, jj * 128:(jj + 1) * 128] for k in range(KC)], [xb[:, k, sl] for k in range(KC)],
                             reads=[cB] + [B_xb[k][nb] for k in range(KC)])
                    mm_group(bh, [hs[:, k, jj * 128:(jj + 1) * 128] for k in range(KC)], [xb[:, k, sl] for k in range(KC)],
                             reads=[hB] + [B_xb[k][nb] for k in range(KC)])
                    s = (j * NB + nb) % 2
                    op(ACT, lambda e: e.activation(out=sg[s][:], in_=bank_ap[bc], func=AF.Copy), reads=[bankB[bc]],
                       writes=[B_sg[s]])
                    op(DVE, lambda e: e.tensor_tensor(out=vh[:, j, 1 + nb * 512:1 + (nb + 1) * 512], in0=sg[s][:],
                                                      in1=bank_ap[bh], op=ALU.mult),
                       reads=[B_sg[s], bankB[bh]], writes=(allhb if first else [B_hball]))
                    first = False
        dma(SP, vsp[t].rearrange("p (j n) -> p j n", j=KC), vh[:, :, 1:T + 1], S_st3, reads=[B_hball], writes=[B_vsp[t]])
        for side in range(2):
            col = 1 if side == 0 else T
            dma(SP, vb_loc[2 * t + side].rearrange("(j p) -> p j", p=128), vh[:, :, col], S_st, reads=[B_hball],
                writes=[B_vbloc], allow_slow_non_contiguous=True)

    def halo_exchange():
        _waits(POOL, [B_vbloc], [B_vbgat])
        ins = POOL.e.collective_compute("AllGather", ALU.bypass, replica_groups=[list(range(NCORES))],
                                        ins=[vb_loc.opt()], outs=[vb_gat.opt()])
        S_cc.count += 1
        ins.then_inc(S_cc.h)
        tok = Tok(S_cc, S_cc.count)
        _record(tok, [B_vbloc], [B_vbgat])
        POOL.wait(tok)
        dma(SP, gst[:], vb_gat[:, :], S_ld2, reads=[B_vbgat], writes=[B_gst])
        b = next_bank()
        for j in range(KC):
            op(PE, lambda e, j=j: e.matmul(bank_ap[b][:, j * 2 * NT:(j + 1) * 2 * NT], lhsT=gst[:, j * 128:(j + 1) * 128],
                                           rhs=hm[:], start=True, stop=True),
               reads=[B_gst, B_const], writes=[bankB[b]], signal=(j == KC - 1))
        op(DVE, lambda e: e.tensor_copy(out=halo[:], in_=bank_ap[b][:, 0:KC * 2 * NT]), reads=[bankB[b]], writes=[B_halo])

    def conv_rest(t, l):
        lc = l // 2
        wi = wb_ci[lc]
        bi = wcast["ci%d" % lc]
        wo = wb_co[lc]
        bo = wcast["co%d" % lc]
        allhb = [B_hball] + [B_hb[f][nb] for f in range(FC) for nb in range(NB)]
        dma(SP, vh[:, :, 1:T + 1], vsp[t].rearrange("p (j n) -> p j n", j=KC), S_x, reads=[B_vsp[t]], writes=allhb)
        hv = halo[:].rearrange("p (j h) -> p j h", j=KC)
        op(DVE, lambda e: e.tensor_copy(out=vh[:, :, 0], in_=hv[:, :, 2 * t]), reads=[B_halo], writes=[B_hball])
        op(DVE, lambda e: e.tensor_copy(out=vh[:, :, T + 1], in_=hv[:, :, 2 * t + 1]), reads=[B_halo], writes=[B_hball])
        for js in range(0, KC, 4):
            bs, bB = load_wa(wi, bi, js * 128, 512)
            for jj in range(4):
                j = js + jj
                ki = (lc * 3) * KC
                for nb in range(NB):
                    sl = slice(nb * 512, (nb + 1) * 512)
                    bb = next_bank()
                    mm_group(bb, [bs[:, k, jj * 128:(jj + 1) * 128] for k in range(KC)], [xb[:, k, sl] for k in range(KC)],
                             reads=[bB] + [B_xb[k][nb] for k in range(KC)])
                    s = (j * NB + nb) % 2
                    c0 = nb * 512
                    op(DVE, lambda e: e.tensor_scalar(out=sg[s][:], in0=vh[:, j, c0:c0 + 512],
                                                      scalar1=convk[:, ki + 0 * KC + j:ki + 0 * KC + j + 1], scalar2=None,
                                                      op0=ALU.mult), reads=[B_hball, B_const], writes=[B_sg[s]])
                    op(DVE, lambda e: e.scalar_tensor_tensor(out=sg[s][:], in0=vh[:, j, c0 + 1:c0 + 513],
                                                             scalar=convk[:, ki + 1 * KC + j:ki + 1 * KC + j + 1], in1=sg[s][:],
                                                             op0=ALU.mult, op1=ALU.add), reads=[B_hball, B_const],
                       writes=[B_sg[s]])
                    op(DVE, lambda e: e.scalar_tensor_tensor(out=sg[s][:], in0=vh[:, j, c0 + 2:c0 + 514],
                                                             scalar=convk[:, ki + 2 * KC + j:ki + 2 * KC + j + 1], in1=sg[s][:],
                                                             op0=ALU.mult, op1=ALU.add), reads=[B_hball, B_const],
                       writes=[B_sg[s]])
                    op(DVE, lambda e: e.tensor_tensor(out=zbuf(j, sl), in0=sg[s][:], in1=bank_ap[bb], op=ALU.mult),
                       reads=[B_sg[s], bankB[bb]], writes=[B_z[j][nb]] + B_wb)
        st = new_ln()
        cnt = 0
        for ms in range(0, KC, 4):
            ws, wB = load_wa(wo, bo, ms * 128, 512)
            for mm in range(4):
                m = ms + mm
                for nb in range(NB):
                    sl = slice(nb * 512, (nb + 1) * 512)
                    b = next_bank()
                    mm_group(b, [ws[:, k, mm * 128:(mm + 1) * 128] for k in range(KC)], [zbuf(k, sl) for k in range(KC)],
                             reads=[wB] + [B_z[k][nb] for k in range(KC)] + B_wb)
                    run_deferred(st, keep=1)
                    residual_from_bank(st, b, m, nb, cnt)
                    cnt += 1
        ln_finish(st, l, 1)

    zt = nc.alloc_sbuf_tensor_at("zt", [128, KC, T], BF16, offset=wbr_off)
    assert KC * T * 2 <= WB_N * FC * 256 * 2
    B_z = [[Buf() for _ in range(NB)] for _ in range(KC)]

    def zbuf(j, sl):
        return zt[:, j, sl]

    def qkv_tile(t, l):
        la = l // 2
        wq = wb_qkv[la]
        bq = wcast["qkv%d" % la]
        wp = wb_qkp[la]
        bp = wcast["qkp%d" % la]
        allhb = [B_hball] + [B_hb[f][nb] for f in range(FC) for nb in range(NB)]
        dma(SP, cos_t[:], cos_in[:, t * T:(t + 1) * T], S_ld2, writes=allhb)
        dma(SP, sin_t[:], sin_in[:, t * T:(t + 1) * T], S_ld3, writes=[B_hball])
        hh = 0
        while hh < NH + NKV:
            n = min(4, NH + NKV - hh)
            ws, wB = load_wa(wq, bq, hh * 128, n * 128)
            ps_, pB = load_wa(wp, bp, hh * 128, n * 128)
            for cc in range(n):
                h = hh + cc
                isq = h < NH
                gcol = la * 4 + (0 if isq else 2)
                for nb in range(NB):
                    sl = slice(nb * 512, (nb + 1) * 512)
                    br, bs_ = next_bank(2)
                    mm_group(br, [ws[:, k, cc * 128:(cc + 1) * 128] for k in range(KC)], [xb[:, k, sl] for k in range(KC)],
                             reads=[wB] + [B_xb[k][nb] for k in range(KC)])
                    mm_group(bs_, [ps_[:, k, cc * 128:(cc + 1) * 128] for k in range(KC)], [xb[:, k, sl] for k in range(KC)],
                             reads=[pB] + [B_xb[k][nb] for k in range(KC)])
                    i = (h * NB + nb) % 2
                    op(ACT, lambda e: e.activation(out=sqb[i][:], in_=bank_ap[br], func=AF.Square), reads=[bankB[br]],
                       writes=[B_sq[i]])
                    bq_ = 4 + (h * NB + nb) % 2
                    op(PE, lambda e: e.matmul(bank_ap[bq_], lhsT=ones_bf[:], rhs=sqb[i][:], start=True, stop=True),
                       reads=[B_sq[i], B_const], writes=[bankB[bq_]])
                    if isq:
                        op(ACT, lambda e: e.activation(out=tmp_t[i][:], in_=bank_ap[bq_], func=AF.Sqrt, bias=epsc[:, 1:2],
                                                       scale=1.0), reads=[bankB[bq_], B_const], writes=[B_tmp[i]])
                    else:
                        op(ACT, lambda e: e.activation(out=tmp_t[i][:], in_=bank_ap[bq_], func=AF.Sqrt, bias=epsc[:, 2:3],
                                                       scale=1.0 / HD), reads=[bankB[bq_], B_const], writes=[B_tmp[i]])
                    op(DVE, lambda e: e.reciprocal(out=tmp_t[i][:], in_=tmp_t[i][:]), reads=[B_tmp[i]], writes=[B_tmp[i]])
                    op(DVE, lambda e: e.scalar_tensor_tensor(out=ropA[:], in0=bank_ap[br], scalar=qkn[:, gcol:gcol + 1],
                                                             in1=cos_t[:, sl], op0=ALU.mult, op1=ALU.mult),
                       reads=[bankB[br], B_const, B_hball], writes=[B_ropA])
                    op(DVE, lambda e: e.scalar_tensor_tensor(out=ropB[:], in0=bank_ap[bs_], scalar=qkn[:, gcol + 1:gcol + 2],
                                                             in1=sin_t[:, sl], op0=ALU.mult, op1=ALU.mult),
                       reads=[bankB[bs_], B_const, B_hball], writes=[B_ropB])
                    op(DVE, lambda e: e.tensor_tensor(out=ropA[:], in0=ropA[:], in1=ropB[:], op=ALU.add),
                       reads=[B_ropB], writes=[B_ropA])
                    dst = qstage[:, h, sl] if isq else kstage[:, h - NH, sl]
                    op(DVE, lambda e: e.tensor_tensor(out=dst, in0=ropA[:], in1=tmp_t[i][:], op=ALU.mult),
                       reads=[B_ropA, B_tmp[i]], writes=[B_qk])
            hh += n
        vs_, vB = load_wa(wq, bq, (NH + NKV) * 128, NKV * HD)
        for g in range(T // 128):
            nb = g // 4
            b = next_bank()
            n = KC
            for k in range(KC):
                op(PE, lambda e, k=k: e.matmul(bank_ap[b][:, 0:NKV * HD], lhsT=xb[:, k, g * 128:(g + 1) * 128],
                                               rhs=vs_[:, k, 0:NKV * HD], start=(k == 0), stop=(k == n - 1)),
                   reads=[vB, B_xb[k][nb]], writes=[bankB[b]], signal=(k == n - 1))
            op(ACT, lambda e: e.activation(out=vstage[:, g, :], in_=bank_ap[b][:, 0:NKV * HD], func=AF.Copy),
               reads=[bankB[b]], writes=[B_qk])
        dma(SP, qsp[t].rearrange("p (h n) -> p h n", h=NH), qstage[:], S_st, reads=[B_qk], writes=[B_qsp[t]])
        dma(SP, kt_loc_v[t * NKV:(t + 1) * NKV].rearrange("k p n -> p k n"), kstage[:], S_st2,
            reads=[B_qk], writes=[B_ktloc])
        dma(SP, v_loc_v[t], vstage[:], S_st3, reads=[B_qk], writes=[B_vloc])
        _record(Tok(S_st, S_st.count), [B_hball], [])
        _record(Tok(S_st2, S_st2.count), [B_hball], [])
        _record(Tok(S_st3, S_st3.count), [B_hball], [])

    B_ropA = Buf()
    B_ropB = Buf()
    B_qk = Buf()

    def kv_exchange():
        for (loc, gat, bl, bg) in [(kt_loc, kt_gat, B_ktloc, B_ktgat), (v_loc, v_gat, B_vloc, B_vgat)]:
            _waits(POOL, [bl], [bg])
            ins = POOL.e.collective_compute("AllGather", ALU.bypass, replica_groups=[list(range(NCORES))],
                                            ins=[loc.opt()], outs=[gat.opt()])
            S_cc.count += 1
            ins.then_inc(S_cc.h)
            _record(Tok(S_cc, S_cc.count), [bl], [bg])
            POOL.wait(Tok(S_cc, S_cc.count))

    def attention(l):
        la = l // 2
        seqs = [list(range(NPT))] + [[NPT + b] for b in range(NSB)]
        ktg = kt_gat.rearrange("(r i) c -> r (i c)", r=NCORES).rearrange("r (q p n) -> p r q n", p=128, n=T)
        vg = v_gat.rearrange("(r i) c -> r (i c)", r=NCORES).rearrange("r (t g p c) -> p r t g c", t=NT, g=T // 128, p=128)
        slot = 0
        pcnt = 0
        acc = 0
        for tiles in seqs:
            nq = len(tiles) * T
            nkeys = NCORES * nq
            nck = nkeys // 128
            for kvh in range(NKV):
                s = slot % 2
                slot += 1
                asub = getattr(cfg, "sub", 0) if _stage == 7 else 0
                for ti, tl in enumerate(tiles):
                    if asub in (11, 13, 14):
                        break
                    dma(SP, KT[s][:, ti * NCORES * T:(ti + 1) * NCORES * T].rearrange("p (r n) -> p r n", r=NCORES),
                        ktg[:, :, tl * NKV + kvh, :], S_KT[s], reads=[B_ktgat], writes=[B_KT[s]], batch=(ti > 0))
                    for r in range(0 if asub == 12 else NCORES):
                        c0 = (ti * NCORES + r) * (T // 128)
                        dma(SP, VS[s][:, c0:c0 + T // 128, :], vg[:, r, tl, :, kvh * HD:(kvh + 1) * HD], S_VS[s],
                            reads=[B_vgat], writes=[B_VS[s]], batch=(ti > 0 or r > 0))
                if asub == 13:
                    for ti, tl in enumerate(tiles):
                        for r in range(NCORES):
                            c0 = (ti * NCORES + r) * (T // 128)
                            dma(SP, VS[s][:, c0:c0 + T // 128, :], vg[:, r, tl, :, kvh * HD:(kvh + 1) * HD], S_VS[s],
                                reads=[B_vgat], writes=[B_VS[s]], batch=(ti > 0 or r > 0))
                for ti, tl in enumerate(tiles):
                    if asub in (11, 12, 13):
                        break
                    dma(SP, qT[:, :, ti * T:(ti + 1) * T],
                        qsp[tl].rearrange("p (h n) -> p h n", h=NH)[:, kvh * GRP:(kvh + 1) * GRP, :], S_qT,
                        reads=[B_qsp[tl]], writes=[B_qT], batch=(ti > 0))
                for qt in range(0 if asub in (1, 11, 12, 13, 14) else nq // 512):
                    qs = slice(qt * 512, (qt + 1) * 512)
                    for hg in range(GRP):
                        a = acc % 2
                        acc += 1
                        bo_, bs_ = 4 + a, 6 + a
                        pend = []
                        for cp in range(nck // 2):
                            pi = pcnt % 2
                            pp = pcnt % PT_N
                            pcnt += 1
                            for i in range(2):
                                ck = cp * 2 + i
                                sig = (i == 1) if asub != 23 else (i == 1 and (cp % 8 == 7 or cp == nck // 2 - 1))
                                op(PE, lambda e, i=i, ck=ck: e.matmul(bank_ap[pi * 2 + i], lhsT=KT[s][:, ck * 128:(ck + 1) * 128],
                                                                     rhs=qT[:, hg, qs], start=True, stop=True),
                                   reads=[B_KT[s], B_qT], writes=[bankB[pi * 2 + i]], signal=sig)
                            while pend:
                                pend.pop(0)()
                            if asub in (21, 23):
                                pass
                            elif asub == 22:
                                for i in range(2):
                                    op(ACT, lambda e, i=i: e.activation(out=pT[pp][:, i, :], in_=bank_ap[pi * 2 + i],
                                                                   func=AF.Exp, bias=negc[:, la:la + 1], scale=1.0),
                                       reads=[bankB[pi * 2 + i], B_negc], writes=[B_pT[pp]])
                            else:
                                op(ACT, lambda e: e.activation(out=pT[pp][:].rearrange("p a n -> p (a n)"), in_=PS[pi][:, :],
                                                               func=AF.Exp, bias=negc[:, la:la + 1], scale=1.0),
                                   reads=[bankB[pi * 2], bankB[pi * 2 + 1], B_negc], writes=[B_pT[pp]])

                            def pv(cp=cp, pp=pp):
                                for i in range(0 if asub in (2, 21, 22, 23) else 2):
                                    ck = cp * 2 + i
                                    last = (ck == nck - 1)
                                    op(PE, lambda e: e.matmul(bank_ap[bo_], lhsT=VS[s][:, ck, :], rhs=pT[pp][:, i, :],
                                                              start=(ck == 0), stop=last),
                                       reads=[B_VS[s], B_pT[pp]], writes=[bankB[bo_]], signal=last)
                                    op(PE, lambda e: e.matmul(bank_ap[bs_], lhsT=ones_bf[:], rhs=pT[pp][:, i, :],
                                                              start=(ck == 0), stop=last),
                                       reads=[B_pT[pp], B_const], writes=[bankB[bs_]], signal=(last or i == 1))
                            pend.append(pv)
                        while pend:
                            pend.pop(0)()
                        if asub in (2, 3, 21, 22, 23):
                            continue
                        op(DVE, lambda e: e.reciprocal(out=rec[a][:], in_=bank_ap[bs_]), reads=[bankB[bs_]], writes=[B_rec[a]])
                        op(DVE, lambda e: e.tensor_tensor(out=oT[:, hg, qs], in0=bank_ap[bo_], in1=rec[a][:], op=ALU.mult),
                           reads=[bankB[bo_], B_rec[a]], writes=[B_oT])
                for ti, tl in enumerate(tiles):
                    if asub in (11, 12, 13, 14):
                        break
                    dma(SP, osp[tl].rearrange("p (h n) -> p h n", h=NH)[:, kvh * GRP:(kvh + 1) * GRP, :],
                        oT[:, :, ti * T:(ti + 1) * T], S_st, reads=[B_oT], writes=[B_osp[tl]], batch=(ti > 0))

    def attn_out(t, l):
        la = l // 2
        wo = wb_ao[la]
        bo = wcast["ao%d" % la]
        allhb = [B_hball] + [B_hb[f][nb] for f in range(FC) for nb in range(NB)]
        dma(SP, oin[:], osp[t].rearrange("p (h n) -> p h n", h=NH), S_ld2, reads=[B_osp[t]], writes=allhb)
        st = new_ln()
        cnt = 0
        for ms in range(0, KC, 4):
            ws, wB = load_wa(wo, bo, ms * 128, 512)
            for mm in range(4):
                m = ms + mm
                for nb in range(NB):
                    sl = slice(nb * 512, (nb + 1) * 512)
                    b = next_bank()
                    mm_group(b, [ws[:, k, mm * 128:(mm + 1) * 128] for k in range(KC)], [oin[:, k, sl] for k in range(KC)],
                             reads=[wB, B_hball])
                    run_deferred(st, keep=1)
                    residual_from_bank(st, b, m, nb, cnt)
                    cnt += 1
        ln_finish(st, l, 1)

    def pe_flush():
        PE.sem.count += 1
        PE.e.nop().then_inc(PE.sem.h, 1)

    def main_program():
        for l in range(DEPTH):
            if l + 1 < DEPTH and l % 2 == 0:
                cast_layer_weights(l + 1)
            if l % 2 == 0:
                for t in range(NT):
                    if l == 0:
                        load_tile_input(t)
                    else:
                        reload_x(t)
                        attn_out(t, l - 1)
                        ffn(l - 1, 2)
                    ffn(l, 1)
                    conv_in(t, l // 2)
                    spill_x(t)
                halo_exchange()
                for t in range(NT):
                    reload_x(t)
                    conv_rest(t, l)
                    ffn(l, 2)
                    if l + 1 < DEPTH:
                        ffn(l + 1, 1)
                        qkv_tile(t, l + 1)
                        spill_x(t)
                    else:
                        store_tile_output(t)
            else:
                if _stage in (5, 6, 7):
                    if _stage >= 6:
                        kv_exchange()
                    if _stage >= 7:
                        barrier()
                        attention(l)
                        barrier()
                    for t in range(NT):
                        reload_x(t)
                        store_tile_output(t)
                    return
                kv_exchange()
                if l + 1 < DEPTH:
                    cast_layer_weights(l + 1)
                barrier()
                attention(l)
                barrier()
                if l == DEPTH - 1:
                    for t in range(NT):
                        reload_x(t)
                        attn_out(t, l)
                        ffn(l, 2)
                        store_tile_output(t)

    stage = getattr(cfg, "stage", 0)
    if stage == -1:
        pass
    elif stage == 1 and getattr(cfg, "sub", 0) > 0:
        sub = cfg.sub
        dma(SP, tmaj[:], xin[0:T, :].rearrange("(g p) d -> p g d", p=128), S_x, writes=[B_hball])
        if sub >= 2:
            b = 0
            for g in range(4):
                op(PE, lambda e, g=g: e.transpose(out=bank_ap[b][:, g * 128:(g + 1) * 128], in_=tmaj[:, g, 0:128], identity=ident[:]),
                   reads=[B_hball, B_const], writes=[bankB[b]], signal=(g == 3))
        if sub >= 3:
            op(ACT, lambda e: e.activation(out=xf[:, 0, 0:512], in_=bank_ap[0], func=AF.Copy), reads=[bankB[0]], writes=[B_xf[0][0]])
        if sub >= 4:
            op(DVE, lambda e: e.tensor_copy(out=xb[:, 0, 0:512], in_=bank_ap[0]), reads=[bankB[0]], writes=[B_xb[0][0]])
        if sub >= 5:
            dma(SP, yout[0:T, :].rearrange("(g p) d -> p g d", p=128), tmaj[:], S_st, reads=[B_hball])
    elif stage in (1, 2):
        for t in range(NT):
            load_tile_input(t)
            if stage == 2:
                ffn(0, 1)
            store_tile_output(t)
    elif stage in (3, 4):
        for t in range(NT):
            load_tile_input(t)
            ffn(0, 1)
            conv_in(t, 0)
            spill_x(t)
        halo_exchange()
        for t in range(NT):
            reload_x(t)
            conv_rest(t, 0)
            if stage == 4:
                ffn(0, 2)
            store_tile_output(t)
    else:
        main_program()
    SP.wait(Tok(S_st, S_st.count))
    fin = []
    for E in (PE, ACT, DVE):
        E.sem.count += 1
        E.e.nop().then_inc(E.sem.h, 1)
        fin.append(Tok(E.sem, E.sem.count))
    for t in fin:
        SP.wait(t)
    for sm in Sem.registry:
        if sm.name.startswith("s_") and sm.count > 0:
            SP.wait(Tok(sm, sm.count))
    S_fin = Sem(nc, "s_fin")
    fin_scr = nc.dram_tensor("fin_scr", [1, 16], F32).ap()
    SP.e.dma_start(out=fin_scr[:, :], in_=ident[0:1, 0:16]).then_inc(S_fin.h, 16)
    S_fin.count += 16
    for t in fin:
        POOL.wait(t)
    for sm in Sem.registry:
        if sm.name.startswith("s_") and sm.count > 0:
            POOL.wait(Tok(sm, sm.count))
    nc.clear_and_free_semaphores(nc.allocated_since(sem_snap))
    nc.all_engine_barrier()
    nc._in_names = in_names
    return nc


def _rope_perm():
    perm = np.zeros(128, np.int64)
    sign = np.zeros(128, np.float32)
    for d in range(128):
        a = d % 64
        base = d - a
        if a < 32:
            perm[d] = base + a + 32
            sign[d] = -1.0
        else:
            perm[d] = base + a - 32
            sign[d] = 1.0
    return perm, sign


def _rope_tables(pos):
    pos = np.asarray(pos, np.int64)
    rows = (pos // GRID_W).astype(np.float32)
    cols = (pos % GRID_W).astype(np.float32)
    inv_freq = (ROPE_THETA ** (-np.arange(0, 64, 2, dtype=np.float32) / np.float32(64))).astype(np.float32)
    perm, sign = _rope_perm()
    cos = np.zeros((128, pos.shape[0]), np.float32)
    sin = np.zeros((128, pos.shape[0]), np.float32)
    for d in range(128):
        p = rows if d < 64 else cols
        ang = (p * inv_freq[d % 32]).astype(np.float32)
        cos[d] = np.cos(ang)
        sin[d] = np.sin(ang) * sign[d]
    return cos, sin


def _feat_major(v):
    v = np.asarray(v, np.float32).reshape(-1, KC, 128)
    return np.ascontiguousarray(v.transpose(2, 0, 1).reshape(128, -1))


_NC_CACHE = {}


def run(cfg, x_prompt, x_sample, ffn1_w_in, ffn1_w_out, ffn2_w_in, ffn2_w_out, ln_g, ln_b, conv_w_in, conv_k, conv_w_out,
        attn_w_qkv, attn_q_norm, attn_k_norm, attn_w_out):
    T, NT, NPT, NSB = cfg.T, cfg.NT, cfg.NPT, cfg.NSB
    key = (cfg.T, cfg.NPT, cfg.NSB, cfg.DEPTH, getattr(cfg, 'stage', 0), getattr(cfg, 'sub', 0))
    if key not in _NC_CACHE:
        _NC_CACHE[key] = build(cfg)
    nc = _NC_CACHE[key]
    perm, sign = _rope_perm()
    f32 = lambda a: np.ascontiguousarray(np.asarray(a, np.float32))
    qk_cols = np.concatenate([h * 128 + perm for h in range(NH + NKV)])
    w_qkp = f32(np.asarray(attn_w_qkv)[:, :, qk_cols])
    NATT = cfg.NATT
    qkn = np.zeros((128, NATT * 4), np.float32)
    qkrow = np.zeros((128, NATT * 256), np.float32)
    for a in range(NATT):
        gq = np.asarray(attn_q_norm[a], np.float32)
        gk = np.asarray(attn_k_norm[a], np.float32)
        qkn[:, a * 4 + 0] = gq
        qkn[:, a * 4 + 1] = gq[perm]
        qkn[:, a * 4 + 2] = gk
        qkn[:, a * 4 + 3] = gk[perm]
        qkrow[:, a * 256:a * 256 + 128] = gq[None, :]
        qkrow[:, a * 256 + 128:a * 256 + 256] = gk[None, :]
    shared = {
        "ffn1_w_in": f32(ffn1_w_in), "ffn1_w_out": f32(ffn1_w_out), "ffn2_w_in": f32(ffn2_w_in),
        "ffn2_w_out": f32(ffn2_w_out), "conv_w_in": f32(conv_w_in), "conv_w_out": f32(conv_w_out),
        "attn_w_qkv": f32(attn_w_qkv), "attn_w_qkp": w_qkp, "attn_w_out": f32(attn_w_out),
        "lng": _feat_major(ln_g), "lnb": _feat_major(ln_b), "convk": _feat_major(conv_k),
        "qkn": qkn, "qkrow": qkrow, "ident": np.eye(128, dtype=np.float32),
    }
    xp = np.asarray(x_prompt, np.float32)
    xs = np.asarray(x_sample, np.float32)
    in_maps = []
    for c in range(NCORES):
        xin = np.concatenate([xp[0, c * NPT * T:(c + 1) * NPT * T]] + [xs[b, c * T:(c + 1) * T] for b in range(NSB)], axis=0)
        pos = np.concatenate([np.arange(c * NPT * T, (c + 1) * NPT * T)] + [np.arange(c * T, (c + 1) * T)] * NSB)
        cos, sin = _rope_tables(pos)
        hm = np.zeros((NCORES * 2 * NT, 2 * NT), np.float32)

        def q(r, t, side):
            return r * 2 * NT + 2 * t + side
        for t in range(NT):
            if t < NPT:
                gl = c * NPT + t - 1
                if gl >= 0:
                    hm[q(gl // NPT, gl % NPT, 1), 2 * t] = 1.0
                gr = c * NPT + t + 1
                if gr < NCORES * NPT:
                    hm[q(gr // NPT, gr % NPT, 0), 2 * t + 1] = 1.0
            else:
                if c > 0:
                    hm[q(c - 1, t, 1), 2 * t] = 1.0
                if c < NCORES - 1:
                    hm[q(c + 1, t, 0), 2 * t + 1] = 1.0
        m = dict(shared)
        m.update({"xin": np.ascontiguousarray(xin), "ropecos": cos, "ropesin": sin, "hmask": hm})
        in_maps.append({k: v for k, v in m.items() if k in nc._in_names})
    res = run_bass_kernel_spmd(nc, in_maps, core_ids=list(range(NCORES)))
    yp = np.zeros((1, cfg.S_PROMPT, D), np.float32)
    ys = np.zeros((NSB, cfg.S_SAMPLE, D), np.float32)
    for c in range(NCORES):
        y = np.asarray(res.results[c]["yout"], np.float32)
        yp[0, c * NPT * T:(c + 1) * NPT * T] = y[0:NPT * T]
        for b in range(NSB):
            ys[b, c * T:(c + 1) * T] = y[(NPT + b) * T:(NPT + b + 1) * T]
    return yp, ys


def kernel(x_prompt, x_sample, ffn1_w_in, ffn1_w_out, ffn2_w_in, ffn2_w_out, ln_g, ln_b, conv_w_in, conv_k, conv_w_out,
           attn_w_qkv, attn_q_norm, attn_k_norm, attn_w_out):
    cfg = Cfg(T=1024, NPT=2, NSB=4, DEPTH=4)
    return run(cfg, x_prompt, x_sample, ffn1_w_in, ffn1_w_out, ffn2_w_in, ffn2_w_out, ln_g, ln_b, conv_w_in, conv_k,
               conv_w_out, attn_w_qkv, attn_q_norm, attn_k_norm, attn_w_out)
```
